# Optimizing a Trainium2 kernel written in Bass

```python
import math
import jax
import jax.numpy as jnp
from jax import lax
import numpy as np

D_MODEL = 1024
BATCH = 8
SEQ = 8192
DEPTH = 4

GRID_W = 64
CTX_LEN = 256
N_MOD = 9
D_FF = 2816
FFN_RES = 0.5
EPS = 1e-6
ROPE_THETA = 10000.0
Q_BLOCK = 128

CONV_WIDTH = 3
D_CONV = 512
D_FOURIER = 512
FOURIER_GROUPS = 4
D_FG = D_FOURIER // FOURIER_GROUPS
D_IN_EVEN = 3 * D_CONV + D_FOURIER
D_OUT_EVEN = D_CONV + D_FOURIER

DA_HEADS = 8
DA_DK = 64
DA_DV = 2 * DA_DK
DA_SCALE = DA_DK ** -0.5
MLA_HEADS = 8
MLA_NOPE = 64
MLA_ROPE = 32
MLA_DQK = MLA_NOPE + MLA_ROPE
MLA_DV = 64
MLA_Q_RANK = 384
MLA_KV_RANK = 256
MLA_SCALE = MLA_DQK ** -0.5
DA_QCOLS = DA_HEADS * 2 * DA_DK
DA_VCOLS = DA_HEADS * DA_DV
Q_COLS = DA_QCOLS + MLA_Q_RANK
KV_COLS = DA_QCOLS + DA_VCOLS + MLA_KV_RANK + MLA_ROPE
D_IN_ODD = Q_COLS + KV_COLS
D_OUT_ODD = DA_HEADS * DA_DV + MLA_HEADS * MLA_DV

kernel_name = 'hybrid_conv_fourier_diffattn_mla_dit'


def rmsnorm(x, g):
    x32 = x.astype(jnp.float32)
    y = x32 * lax.rsqrt(jnp.mean(x32 * x32, axis=-1, keepdims=True) + EPS)
    return (y * g.astype(jnp.float32)).astype(x.dtype)


def modulate(h, g, shift, scale):
    return rmsnorm(h, g) * (1 + scale) + shift


def adaln_params(cond, w, b):
    m = jax.nn.silu(cond) @ w + b
    return m.reshape(cond.shape[0], 1, N_MOD, cond.shape[-1])


def swiglu(x, w_in, w_out):
    gate, up = jnp.split(x @ w_in, 2, axis=-1)
    return (jax.nn.silu(gate) * up) @ w_out


def macaron_ffn(h, m, g, w_in, w_out, slot):
    xm = modulate(h, g[2 * slot], m[:, :, 3 * slot], m[:, :, 3 * slot + 1])
    y = swiglu(xm, w_in, w_out)
    return h + FFN_RES * m[:, :, 3 * slot + 2] * rmsnorm(y, g[2 * slot + 1])


def rope_tables(n_tokens, rot_dim):
    rows = n_tokens // GRID_W
    row = jnp.broadcast_to(jnp.arange(rows)[:, None], (rows, GRID_W)).reshape(-1)
    col = jnp.broadcast_to(jnp.arange(GRID_W)[None, :], (rows, GRID_W)).reshape(-1)
    n_freq = rot_dim // 4
    freqs = ROPE_THETA ** (-jnp.arange(n_freq, dtype=jnp.float32) / n_freq)
    ang = jnp.stack([row.astype(jnp.float32)[:, None] * freqs,
                     col.astype(jnp.float32)[:, None] * freqs], axis=1)
    ang = ang[None, :, None, :, None, :]
    return jnp.cos(ang), jnp.sin(ang)


def apply_rope(x, cos, sin):
    b_, n, h, r = x.shape
    xr = x.astype(jnp.float32).reshape(b_, n, h, 2, 2, r // 4)
    x1, x2 = xr[..., 0:1, :], xr[..., 1:2, :]
    out = jnp.concatenate([x1 * cos - x2 * sin, x2 * cos + x1 * sin], axis=-2)
    return out.reshape(b_, n, h, r).astype(x.dtype)


def short_conv_fourier_mixer(xm, w_in, conv_w, w_out):
    b_, t = xm.shape[:2]
    u = xm @ w_in
    gate_b, gate_c, xv, xf = jnp.split(u, [D_CONV, 2 * D_CONV, 3 * D_CONV], axis=-1)
    pad = CONV_WIDTH // 2
    z = jnp.pad(gate_c * xv, ((0, 0), (pad, pad), (0, 0)))
    conv = z[:, 0:t] * conv_w[0]
    for k in range(1, CONV_WIDTH):
        conv = conv + z[:, k:k + t] * conv_w[k]
    y_conv = gate_b * conv
    xf = xf.astype(jnp.float32).reshape(b_, t, FOURIER_GROUPS, D_FG)
    y_four = jnp.real(jnp.fft.fft2(xf, axes=(1, 3), norm='ortho'))
    y_four = y_four.reshape(b_, t, D_FOURIER).astype(xm.dtype)
    return jnp.concatenate([y_conv, y_four], axis=-1) @ w_out


def attn_queries(u_q, g_q, w_uq, rope_da, rope_mla):
    b_, t = u_q.shape[:2]
    q_da, c_q = jnp.split(u_q, [DA_QCOLS], axis=-1)
    q_da = q_da.reshape(b_, t, DA_HEADS * 2, DA_DK)
    q_m = (rmsnorm(c_q, g_q) @ w_uq).reshape(b_, t, MLA_HEADS, MLA_DQK)
    q_nope, q_rope = q_m[..., :MLA_NOPE], q_m[..., MLA_NOPE:]
    if rope_da is not None:
        q_da = apply_rope(q_da, *rope_da)
        q_rope = apply_rope(q_rope, *rope_mla)
    q_da = q_da.reshape(b_, t, DA_HEADS, 2, DA_DK)
    return q_da, jnp.concatenate([q_nope, q_rope], axis=-1)


def attn_keys_values(u_kv, g_kv, w_uk, w_uv, rope_da, rope_mla):
    b_, t = u_kv.shape[:2]
    k_da, v_da, c_kv, k_r = jnp.split(
        u_kv, [DA_QCOLS, DA_QCOLS + DA_VCOLS, DA_QCOLS + DA_VCOLS + MLA_KV_RANK], axis=-1)
    k_da = k_da.reshape(b_, t, DA_HEADS * 2, DA_DK)
    v_da = v_da.reshape(b_, t, DA_HEADS, DA_DV)
    c_kv = rmsnorm(c_kv, g_kv)
    k_nope = (c_kv @ w_uk).reshape(b_, t, MLA_HEADS, MLA_NOPE)
    v_m = (c_kv @ w_uv).reshape(b_, t, MLA_HEADS, MLA_DV)
    k_r = k_r.reshape(b_, t, 1, MLA_ROPE)
    if rope_da is not None:
        k_da = apply_rope(k_da, *rope_da)
        k_r = apply_rope(k_r, *rope_mla)
    k_da = k_da.reshape(b_, t, DA_HEADS, 2, DA_DK)
    k_m = jnp.concatenate([k_nope, jnp.broadcast_to(k_r, (b_, t, MLA_HEADS, MLA_ROPE))], axis=-1)
    return k_da, v_da, k_m, v_m


def attend(q_da, q_m, k_da, v_da, k_m, v_m, lam):
    f32 = jnp.float32
    s = jnp.einsum('bqhmd,bkhmd->bhmqk', q_da, k_da).astype(f32) * DA_SCALE
    p = jax.nn.softmax(s, axis=-1)
    w_diff = p[:, :, 0] - lam * p[:, :, 1]
    o_da = jnp.einsum('bhqk,bkhd->bqhd', w_diff, v_da.astype(f32))
    sm = jnp.einsum('bqhd,bkhd->bhqk', q_m, k_m).astype(f32) * MLA_SCALE
    pm = jax.nn.softmax(sm, axis=-1)
    o_m = jnp.einsum('bhqk,bkhd->bqhd', pm, v_m.astype(f32))
    return o_da.astype(q_da.dtype), o_m.astype(q_m.dtype)


def attend_latent_blocks(qs, kvs, lam):
    b_, t = qs[0].shape[:2]
    nb = t // Q_BLOCK
    blocks = tuple(jnp.moveaxis(q.reshape(b_, nb, Q_BLOCK, *q.shape[2:]), 1, 0) for q in qs)
    o_da, o_m = lax.map(lambda qb: attend(qb[0], qb[1], *kvs, lam), blocks)
    unblock = lambda o: jnp.moveaxis(o, 0, 1).reshape(b_, t, *o.shape[3:])
    return unblock(o_da), unblock(o_m)


def attn_output(o_da, o_m, g_sub, lam_init, w_out):
    b_, t = o_da.shape[:2]
    o_da = (rmsnorm(o_da, g_sub) * (1 - lam_init)).reshape(b_, t, DA_HEADS * DA_DV)
    return jnp.concatenate([o_da, o_m.reshape(b_, t, MLA_HEADS * MLA_DV)], axis=-1) @ w_out


def setup_inputs(seed: int = 0) -> dict:
    key = jax.random.key(seed)
    ks = iter(jax.random.split(key, 32))
    f32 = jnp.float32
    n_even = (DEPTH + 1) // 2
    n_odd = DEPTH // 2

    def nrm(shape, scale):
        return jax.random.normal(next(ks), shape, f32) * scale

    def gain(shape):
        return 1.0 + nrm(shape, 0.05)

    return {
        'x': nrm((BATCH, SEQ, D_MODEL), 1.0),
        'c': nrm((BATCH, D_MODEL), 1.0),
        'ctx': nrm((BATCH, CTX_LEN, D_MODEL), 1.0),
        'c_ctx': nrm((D_MODEL,), 1.0),
        'w_mod': nrm((DEPTH, D_MODEL, N_MOD * D_MODEL), 0.5 * D_MODEL ** -0.5),
        'b_mod': nrm((DEPTH, N_MOD * D_MODEL), 0.01),
        'norm_g': gain((DEPTH, 6, D_MODEL)),
        'w_ffn_in': nrm((DEPTH, 2, D_MODEL, 2 * D_FF), D_MODEL ** -0.5),
        'w_ffn_out': nrm((DEPTH, 2, D_FF, D_MODEL), D_FF ** -0.5),
        'w_in_even': nrm((n_even, D_MODEL, D_IN_EVEN), D_MODEL ** -0.5),
        'conv_w': nrm((n_even, CONV_WIDTH, D_CONV), CONV_WIDTH ** -0.5),
        'w_out_even': nrm((n_even, D_OUT_EVEN, D_MODEL), D_OUT_EVEN ** -0.5),
        'w_in_odd': nrm((n_odd, D_MODEL, D_IN_ODD), D_MODEL ** -0.5),
        'g_q_mla': gain((n_odd, MLA_Q_RANK)),
        'w_uq': nrm((n_odd, MLA_Q_RANK, MLA_HEADS * MLA_DQK), MLA_Q_RANK ** -0.5),
        'g_kv_mla': gain((n_odd, MLA_KV_RANK)),
        'w_uk': nrm((n_odd, MLA_KV_RANK, MLA_HEADS * MLA_NOPE), MLA_KV_RANK ** -0.5),
        'w_uv': nrm((n_odd, MLA_KV_RANK, MLA_HEADS * MLA_DV), MLA_KV_RANK ** -0.5),
        'lam_q1': nrm((n_odd, DA_DK), 0.1),
        'lam_k1': nrm((n_odd, DA_DK), 0.1),
        'lam_q2': nrm((n_odd, DA_DK), 0.1),
        'lam_k2': nrm((n_odd, DA_DK), 0.1),
        'g_subln': gain((n_odd, DA_DV)),
        'w_out_odd': nrm((n_odd, D_OUT_ODD, D_MODEL), D_OUT_ODD ** -0.5),
    }


def reference(x, c, ctx, c_ctx, w_mod, b_mod, norm_g, w_ffn_in, w_ffn_out,
              w_in_even, conv_w, w_out_even,
              w_in_odd, g_q_mla, w_uq, g_kv_mla, w_uk, w_uv,
              lam_q1, lam_k1, lam_q2, lam_k2, g_subln, w_out_odd):
    f32 = jnp.float32
    n_lat = x.shape[1]
    rope_da = rope_tables(n_lat, DA_DK)
    rope_mla = rope_tables(n_lat, MLA_ROPE)
    h, hc = x, ctx
    for l in range(DEPTH):
        last = l == DEPTH - 1
        odd = l % 2 == 1
        ctx_live = (not last) or odd
        g = norm_g[l]
        m_x = adaln_params(c, w_mod[l], b_mod[l])
        h = macaron_ffn(h, m_x, g, w_ffn_in[l, 0], w_ffn_out[l, 0], 0)
        if ctx_live:
            m_c = adaln_params(c_ctx[None, :], w_mod[l], b_mod[l])
            hc = macaron_ffn(hc, m_c, g, w_ffn_in[l, 0], w_ffn_out[l, 0], 0)

        if not odd:
            e = l // 2
            xm = modulate(h, g[2], m_x[:, :, 3], m_x[:, :, 4])
            y = short_conv_fourier_mixer(xm, w_in_even[e], conv_w[e], w_out_even[e])
            h = h + m_x[:, :, 5] * rmsnorm(y, g[3])
            if ctx_live:
                xc = modulate(hc, g[2], m_c[:, :, 3], m_c[:, :, 4])
                yc = short_conv_fourier_mixer(xc, w_in_even[e], conv_w[e], w_out_even[e])
                hc = hc + m_c[:, :, 5] * rmsnorm(yc, g[3])
        else:
            o = l // 2
            lam_init = 0.8 - 0.6 * math.exp(-0.3 * l)
            lam = (jnp.exp(jnp.sum(lam_q1[o].astype(f32) * lam_k1[o].astype(f32)))
                   - jnp.exp(jnp.sum(lam_q2[o].astype(f32) * lam_k2[o].astype(f32))) + lam_init)
            w_in = w_in_odd[o]
            xm = modulate(h, g[2], m_x[:, :, 3], m_x[:, :, 4])
            xc = modulate(hc, g[2], m_c[:, :, 3], m_c[:, :, 4])
            u = xm @ w_in
            q_x = attn_queries(u[..., :Q_COLS], g_q_mla[o], w_uq[o], rope_da, rope_mla)
            kv_x = attn_keys_values(u[..., Q_COLS:], g_kv_mla[o], w_uk[o], w_uv[o], rope_da, rope_mla)
            kv_c = attn_keys_values(xc @ w_in[:, Q_COLS:], g_kv_mla[o], w_uk[o], w_uv[o], None, None)
            kv_all = tuple(jnp.concatenate([kc, kx], axis=1) for kc, kx in zip(kv_c, kv_x))
            o_da, o_m = attend_latent_blocks(q_x, kv_all, lam)
            y = attn_output(o_da, o_m, g_subln[o], lam_init, w_out_odd[o])
            if not last:
                q_c = attn_queries(xc @ w_in[:, :Q_COLS], g_q_mla[o], w_uq[o], None, None)
                oc_da, oc_m = attend(q_c[0], q_c[1], *kv_c, lam)
                yc = attn_output(oc_da, oc_m, g_subln[o], lam_init, w_out_odd[o])
                hc = hc + m_c[:, :, 5] * rmsnorm(yc, g[3])
            h = h + m_x[:, :, 5] * rmsnorm(y, g[3])

        h = macaron_ffn(h, m_x, g, w_ffn_in[l, 1], w_ffn_out[l, 1], 2)
        if not last:
            hc = macaron_ffn(hc, m_c, g, w_ffn_in[l, 1], w_ffn_out[l, 1], 2)
    return h
```

```python
import contextlib
import math
import numpy as np
import concourse.bass as bass
import concourse.mybir as mybir

F32 = mybir.dt.float32
BF16 = mybir.dt.bfloat16
AF = mybir.ActivationFunctionType
ALU = mybir.AluOpType
AX = mybir.AxisListType

ENGS = ("pe", "act", "dve", "pool", "sp")


class Buf:
    __slots__ = ("name", "last_w", "readers", "dsem")

    def __init__(self, name):
        self.name = name
        self.last_w = None
        self.readers = []
        self.dsem = None


class DSem:
    __slots__ = ("h", "count", "dlast")

    def __init__(self, h):
        self.h = h
        self.count = 0
        self.dlast = None


class Op:
    __slots__ = ("eng", "fn", "pos", "waits", "signal", "is_dma", "dsem", "dval", "sigval", "phase")

    def __init__(self, eng, fn):
        self.phase = None
        self.eng = eng
        self.fn = fn
        self.pos = None
        self.waits = []
        self.signal = False
        self.is_dma = False
        self.dsem = None
        self.dval = 0
        self.sigval = 0


class Sched:
    def __init__(self, nc, same_engine_sync=True):
        self.nc = nc
        self.q = {e: [] for e in ENGS}
        self.seen = {e: {} for e in ENGS}
        self.same_engine_sync = same_engine_sync
        self.sem_bufs = []
        self.nops = 0
        self.stack = None
        self.phase = 0
        self.free_dsems = []
        self.all_dsems = []

    def _dep(self, op, prod, raw=False):
        if prod is None or prod is op or prod.phase != self.phase:
            return
        if prod.is_dma:
            key = ("d", id(prod.dsem))
            val = prod.dval
        else:
            if prod.eng == op.eng and not (raw and self.same_engine_sync and op.eng != "pe"):
                return
            key = ("e", prod.eng)
            val = prod.pos
        seen = self.seen[op.eng]
        if seen.get(key, -1) >= val:
            return
        seen[key] = val
        if not prod.is_dma:
            prod.signal = True
        op.waits.append(prod)

    def _track(self, op, reads, writes):
        for b in reads:
            self._dep(op, b.last_w, raw=True)
        for b in writes:
            self._dep(op, b.last_w)
            for r in b.readers:
                self._dep(op, r)
        for b in reads:
            b.readers.append(op)
        for b in writes:
            b.last_w = op
            b.readers = []

    def op(self, eng, fn, reads=(), writes=()):
        o = Op(eng, fn)
        o.phase = self.phase
        o.pos = len(self.q[eng])
        self._track(o, reads, writes)
        self.q[eng].append(o)
        self.nops += 1
        return o

    def dma(self, eng, out_ap, in_ap, reads, writes, sem_buf, **kw):
        pairs = out_ap if isinstance(out_ap, list) else [(out_ap, in_ap)]
        if sem_buf.dsem is None:
            if self.free_dsems:
                sem_buf.dsem = self.free_dsems.pop()
            else:
                sem_buf.dsem = DSem(self.stack.enter_context(self.nc.semaphore("ds%d" % len(self.all_dsems))))
                self.all_dsems.append(sem_buf.dsem)
            self.sem_bufs.append(sem_buf)
        ds = sem_buf.dsem
        o = Op(eng, None)
        o.phase = self.phase
        o.is_dma = True
        o.dsem = ds
        o.pos = len(self.q[eng])
        self._dep(o, ds.dlast)
        self._track(o, reads, writes)
        ds.count += 16 * len(pairs)
        o.dval = ds.count
        ds.dlast = o
        o.fn = (pairs, kw)
        self.q[eng].append(o)
        self.nops += 1
        return o

    def emit(self, final_waits=()):
        nc = self.nc
        import contextlib
        if not hasattr(self, "esem"):
            self.esem = {e: self.stack.enter_context(nc.semaphore("es_" + e)) for e in ENGS}
            self.sigbase = {e: 0 for e in ENGS}
        esem = self.esem
        with contextlib.ExitStack() as st:
            for e in ENGS:
                c = self.sigbase[e]
                for o in reversed(self.q[e]):
                    if not o.is_dma:
                        o.signal = True
                        break
                for o in self.q[e]:
                    if o.signal:
                        c += 1
                    o.sigval = c
                self.sigbase[e] = c
            block = st.enter_context(nc.Block())

            def run(e, eng):
                for o in self.q[e]:
                    for p in o.waits:
                        if p.is_dma:
                            eng.wait_ge(p.dsem.h, p.dval)
                        else:
                            eng.wait_ge(esem[p.eng], p.sigval)
                    if o.is_dma:
                        pairs, kw = o.fn
                        for (oa, ia) in pairs:
                            eng.dma_start(out=oa, in_=ia, **kw).then_inc(o.dsem.h, 16)
                    else:
                        ins = o.fn(eng)
                        if o.signal:
                            ins.then_inc(esem[e], 1)
                for e2 in ENGS:
                    if e2 != e and self.sigbase[e2] > 0:
                        eng.wait_ge(esem[e2], self.sigbase[e2])
                for b in self.sem_bufs:
                    eng.wait_ge(b.dsem.h, b.dsem.count)

            @block.tensor
            def _(eng):
                run("pe", eng)

            @block.scalar
            def _(eng):
                run("act", eng)

            @block.vector
            def _(eng):
                run("dve", eng)

            @block.gpsimd
            def _(eng):
                run("pool", eng)

            @block.sync
            def _(eng):
                run("sp", eng)
        self.q = {e: [] for e in ENGS}
        self.seen = {e: {} for e in ENGS}
        self.phase += 1
        for b in self.sem_bufs:
            self.free_dsems.append(b.dsem)
            b.dsem = None
        self.sem_bufs = []

D = 1024
SEQ = 8192
CTX = 256
NTOK = SEQ + CTX
DFF = 2816
NJ = DFF // 128
T = 256
NT = NTOK // T
EPS = 1e-6
DEPTH = 4


def MM(S, out, lhsT, rhs, start, stop, rd, wr):
    return S.op("pe", lambda e: e.matmul(out=out, lhsT=lhsT, rhs=rhs, start=start, stop=stop), rd, wr)


def TR(S, out, in_, ident, rd, wr):
    return S.op("pe", lambda e: e.transpose(out=out, in_=in_, identity=ident), rd, wr)


def ACTV(S, out, in_, func, rd, wr, **kw):
    return S.op("act", lambda e: e.activation(out=out, in_=in_, func=func, **kw), rd, wr)


def TT(S, eng, out, in0, in1, op, rd, wr):
    return S.op(eng, lambda e: e.tensor_tensor(out=out, in0=in0, in1=in1, op=op), rd, wr)


def TS(S, eng, out, in0, s1, s2, op0, op1, rd, wr):
    if s2 is None:
        return S.op(eng, lambda e: e.tensor_scalar(out=out, in0=in0, scalar1=s1, scalar2=None, op0=op0), rd, wr)
    return S.op(eng, lambda e: e.tensor_scalar(out=out, in0=in0, scalar1=s1, scalar2=s2, op0=op0, op1=op1), rd, wr)


def STT(S, eng, out, in0, scalar, in1, op0, op1, rd, wr):
    return S.op(eng, lambda e: e.scalar_tensor_tensor(out=out, in0=in0, scalar=scalar, in1=in1, op0=op0, op1=op1),
                rd, wr)


def CP(S, eng, out, in_, rd, wr):
    if eng == "act":
        return S.op("act", lambda e: e.activation(out=out, in_=in_, func=AF.Copy), rd, wr)
    return S.op(eng, lambda e: e.tensor_copy(out=out, in_=in_), rd, wr)


def RECIP(S, out, in_, rd, wr):
    return S.op("dve", lambda e: e.reciprocal(out=out, in_=in_), rd, wr)


def MEMSET(S, eng, ap, val, wr):
    return S.op(eng, lambda e: e.memset(ap, val), [], wr)


class Phase:
    cnt = 0

    def __init__(self, kb):
        self.kb = kb
        self.st = contextlib.ExitStack()

    def __enter__(self):
        self.st.__enter__()
        return self

    def __exit__(self, *a):
        if a[0] is None:
            self.kb.S.emit()
        return self.st.__exit__(*a)

    def sb(self, name, shape, dt):
        Phase.cnt += 1
        nm = "%s_%d" % (name, Phase.cnt)
        return self.st.enter_context(self.kb.nc.sbuf_tensor(nm, list(shape), dt)), Buf(nm)

    def ps(self, name, shape, dt):
        Phase.cnt += 1
        nm = "%s_%d" % (name, Phase.cnt)
        return self.st.enter_context(self.kb.nc.psum_tensor(nm, list(shape), dt)), Buf(nm)


class KB:
    def __init__(self, stop_after=None):
        self.nc = bass.Bass("TRN2", target_bir_lowering=False)
        self.S = Sched(self.nc)
        self.stop_after = stop_after
        self.dbufs = {}

    def din(self, name, shape, dt=F32):
        return self.nc.dram_tensor(name, list(shape), dt, kind="ExternalInput").ap()

    def dscr(self, name, shape, dt):
        return self.nc.dram_tensor(name, list(shape), dt, kind="Internal").ap()

    def dout(self, name, shape, dt=F32):
        return self.nc.dram_tensor(name, list(shape), dt, kind="ExternalOutput").ap()

    def db(self, key):
        b = self.dbufs.get(key)
        if b is None:
            b = self.dbufs[key] = Buf(str(key))
        return b

    def build(self):
        nc, S = self.nc, self.S
        I = self.I = {}
        I["x"] = self.din("x", [SEQ, D])
        I["ctx"] = self.din("ctx", [CTX, D])
        I["cc"] = self.din("cc", [2, D])
        I["w_mod"] = self.din("w_mod", [DEPTH, D, 9 * D])
        I["b_mod"] = self.din("b_mod", [DEPTH, 9 * D])
        I["norm_g"] = self.din("norm_g", [DEPTH, 6 * D])
        I["w_ffn_in"] = self.din("w_ffn_in", [DEPTH, 2, D, 2 * DFF])
        I["w_ffn_out"] = self.din("w_ffn_out", [DEPTH, 2, DFF, D])
        I["w_in_even"] = self.din("w_in_even", [2, D, 2048])
        I["conv_w"] = self.din("conv_w", [2, 3, 512])
        I["w_out_even"] = self.din("w_out_even", [2, D, D])
        I["CD"] = self.din("CD", [128, 2, 128])
        I["F1c"] = self.din("F1c", [128, 128, 128])
        I["C2S2"] = self.din("C2S2", [128, 2, 128])
        I["C256"] = self.din("C256", [128, 2, 2, 256])
        I["w_in_odd"] = self.din("w_in_odd", [2, D, 3744])
        I["g_q_mla"] = self.din("g_q_mla", [2, 384])
        I["w_uq"] = self.din("w_uq", [2, 384, 768])
        I["g_kv_mla"] = self.din("g_kv_mla", [2, 256])
        I["w_uk"] = self.din("w_uk", [2, 256, 512])
        I["w_uv"] = self.din("w_uv", [2, 256, 512])
        I["lamv"] = self.din("lamv", [2, 4 * 64])
        I["g_subln"] = self.din("g_subln", [2, 128])
        I["w_out_odd"] = self.din("w_out_odd", [2, 1536, D])
        I["rope_da"] = self.din("rope_da", [2, 128, NTOK])
        I["rope_m"] = self.din("rope_m", [2, 96, NTOK])
        self.Qda = self.dscr("Qda", [8, 128, NTOK], BF16)
        self.Kda = self.dscr("Kda", [8, 128, NTOK], BF16)
        self.Vda = self.dscr("Vda", [NTOK, 1024], BF16)
        self.Qm = self.dscr("Qm", [8, 96, NTOK], BF16)
        self.Km = self.dscr("Km", [8, 96, NTOK], BF16)
        self.Vm = self.dscr("Vm", [NTOK, 512], BF16)
        self.Od = self.dscr("Od", [1536, NTOK], BF16)
        self.Wd = self.dscr("Wd", [2, NTOK, 512], BF16)
        self.Bd = self.dscr("Bd", [2, 64, 128, 512], BF16)
        self.Yd = self.dscr("Yd", [NTOK, 512], BF16)
        self.Zd = self.dscr("Zd", [512, NTOK + 4], BF16)
        self.Gd = self.dscr("Gd", [512, NTOK], BF16)
        self.out = self.dout("out", [SEQ, D])
        self.H = self.dscr("H", [NTOK, D], F32)
        self.Mrow = self.dscr("Mrow", [DEPTH, 2, 9 * D], F32)
        with contextlib.ExitStack() as st0:
            S.stack = st0
            self.ident = st0.enter_context(nc.sbuf_tensor("ident", [128, 128], BF16))
            self.identf = st0.enter_context(nc.sbuf_tensor("identf", [128, 128], F32))
            self.epsc = st0.enter_context(nc.sbuf_tensor("epsc", [128, 1], F32))
            self.Bconst = Buf("const")
            self.phase_init()
            if self.stop_after == "init":
                return self.finish()
            for l in range(DEPTH):
                self.phase_ffn(l, 0)
                if self.stop_after == "L%d_ffn1" % l:
                    return self.finish()
                if l % 2 == 0:
                    self.phase_even_in(l)
                    self.phase_f1()
                    self.phase_f3()
                    self.phase_even_out(l)
                else:
                    self.phase_odd_in(l)
                    self.phase_attn(l)
                    self.phase_odd_out(l)
                if self.stop_after == "L%d_mix" % l:
                    return self.finish()
                self.phase_ffn(l, 1, skip_ctx=(l == DEPTH - 1))
                if self.stop_after == "L%d_ffn2" % l:
                    return self.finish()
            self.finish()
        return nc

    def finish(self):
        S = self.S
        if not getattr(self, "final_written", False):
            with Phase(self) as P:
                for i in range(8):
                    S.dma("sp", self.out[i * 1024:(i + 1) * 1024, :], self.H[CTX + i * 1024:CTX + (i + 1) * 1024, :],
                          [self.db(("H", "all"))], [self.db(("out", i))], self.db(("out", i)))
        return self.nc

    def phase_init(self):
        nc, S, I = self.nc, self.S, self.I
        with Phase(self) as P:
            Bc = self.Bconst
            MEMSET(S, "pool", self.identf[:], 1.0, [Bc])
            S.op("pool", lambda e: e.affine_select(out=self.identf[:], in_=self.identf[:], pattern=[[-1, 128]],
                                                   compare_op=ALU.is_equal, fill=0.0, base=0,
                                                   channel_multiplier=1), [Bc], [Bc])
            CP(S, "pool", self.ident[:], self.identf[:], [Bc], [Bc])
            MEMSET(S, "pool", self.epsc[:], EPS, [Bc])
            Hall = self.db(("H", "all"))
            S.dma("sp", self.H[0:CTX, :], I["ctx"], [], [Hall], Hall)
            for i in range(8):
                S.dma("sp", self.H[CTX + i * 1024:CTX + (i + 1) * 1024, :], I["x"][i * 1024:(i + 1) * 1024, :],
                      [], [Hall], Hall)
            zz, Bzz = P.sb("zz", [128, 4, 2], BF16)
            MEMSET(S, "pool", zz[:], 0.0, [Bzz])
            Zv = self.Zd.rearrange("(cc p) t -> p cc t", p=128)
            for (c0, w) in ((0, 1), (257, 2), (NTOK + 3, 1)):
                S.dma("sp", Zv[:, :, c0:c0 + w], zz[:, :, 0:w], [Bzz], [self.db(("Zd", "pad"))], Bzz,
                      allow_slow_non_contiguous=True)
            scT, Bsc = P.sb("scT", [128, 8, 2], F32)
            for r in range(2):
                S.dma("sp", scT[:, :, r], I["cc"][r].rearrange("(k p) -> p k", p=128), [], [Bsc], Bsc,
                      allow_slow_non_contiguous=True)
            ACTV(S, scT[:], scT[:], AF.Silu, [Bsc], [Bsc])
            wm = [P.sb("wm%d" % i, [128, 8, 1024], F32) for i in range(2)]
            bm, Bbm = P.sb("bm", [2, 9 * D], F32)
            ng, Bng = P.sb("ng", [2, 6 * D], F32)
            mr, Bmr = P.sb("mr", [2, 9 * D], F32)
            Dt, BD = P.sb("Dt", [2, 9 * D], F32)
            pm = [P.ps("pm%d" % i, [2, 1024], F32) for i in range(2)]
            for l in range(DEPTH):
                S.dma("sp", bm[:], I["b_mod"][l:l + 1, :].partition_broadcast(2), [], [Bbm], Bbm)
                S.dma("sp", ng[:], I["norm_g"][l:l + 1, :].partition_broadcast(2), [], [Bng], Bng)
                for v in range(9):
                    i = (l * 9 + v) % 2
                    wt, Bw = wm[i]
                    pt, Bp = pm[i]
                    S.dma("sp", wt[:], I["w_mod"][l][:, v * D:(v + 1) * D].rearrange("(k p) n -> p k n", p=128),
                          [], [Bw], Bw)
                    for half in range(2):
                        for k in range(8):
                            MM(S, pt[:, half * 512:(half + 1) * 512], scT[:, k, :], wt[:, k, half * 512:(half + 1) * 512],
                               k == 0, k == 7, [Bsc, Bw], [Bp])
                    TT(S, "dve", mr[:, v * D:(v + 1) * D], pt[:], bm[:, v * D:(v + 1) * D], ALU.add, [Bp, Bbm], [Bmr])
                for s in range(3):
                    coef = 1.0 if s == 1 else 0.5
                    sl = lambda j: slice(j * D, (j + 1) * D)
                    CP(S, "dve", Dt[:, sl(3 * s)], mr[:, sl(3 * s)], [Bmr], [BD])
                    STT(S, "dve", Dt[:, sl(3 * s + 1)], mr[:, sl(3 * s + 1)], 1.0, ng[:, sl(2 * s)], ALU.add, ALU.mult,
                        [Bmr, Bng], [BD])
                    STT(S, "dve", Dt[:, sl(3 * s + 2)], mr[:, sl(3 * s + 2)], coef, ng[:, sl(2 * s + 1)], ALU.mult, ALU.mult,
                        [Bmr, Bng], [BD])
                S.dma("sp", self.Mrow[l], Dt[:], [BD], [self.db(("Mrow", l))], BD)

    def row_common(self, P, l, slot, with_epi=True, with_prep=True):
        R = type("R", (), {})()
        R.mods = [P.sb("mod%d" % i, [128, D], F32) for i in range(3)]
        R.hs = [[P.sb("hs%d%d" % (b, s), [128, D], F32) for s in range(2)] for b in range(2)]
        R.tmp = P.sb("tmp", [128, D], F32)
        if with_prep:
            R.xmb = [P.sb("xmb%d" % s, [128, D], BF16) for s in range(2)]
            R.xmT = [P.sb("xmT%d" % b, [128, 8, T], BF16) for b in range(2)]
        if with_epi:
            R.ho = [P.sb("ho%d" % s, [128, D], F32) for s in range(2)]
        R.st = [[P.sb("st%d%d" % (b, s), [128, 4], F32) for s in range(2)] for b in range(2)]
        R.st2 = [P.sb("stb%d" % s, [128, 4], F32) for s in range(2)]
        R.pT = P.ps("pT", [128, 8, 128], BF16)
        if with_epi:
            R.py = [P.ps("py%d" % s, [128, D], F32) for s in range(2)]
        R.l, R.slot = l, slot
        R.cur_mod = None
        return R

    def load_mods(self, R, r):
        S = self.S
        if R.cur_mod == r:
            return
        R.cur_mod = r
        for i in range(3):
            t, B = R.mods[i]
            col = (3 * R.slot + i) * D
            S.dma("sp", t[:], self.Mrow[R.l, r:r + 1, col:col + D].partition_broadcast(128),
                  [self.db(("Mrow", R.l))], [B], B)

    def tile_rows(self, ti, s):
        r0 = ti * T + s * 128
        return r0

    def prep_steps(self, R, ti, bi):
        S = self.S
        (sh, Bsh), (G, BG), _ = R.mods
        tmp, Btmp = R.tmp
        xmT, BxmT = R.xmT[bi]
        pT, BpT = R.pT
        for s in range(2):
            hs, Bhs = R.hs[bi][s]
            st, Bst = R.st[bi][s]
            xmb, Bxmb = R.xmb[s]
            r0 = self.tile_rows(ti, s)
            S.dma("sp", hs[:], self.H[r0:r0 + 128, :], [self.db(("H", "all")), self.db(("H", ti))], [Bhs], Bhs)
            yield
            ACTV(S, xmb[:], hs[:], AF.Square, [Bhs], [Bxmb, Bst], accum_out=st[:, 0:1])
            ACTV(S, st[:, 1:2], st[:, 0:1], AF.Sqrt, [Bst], [Bst], scale=1.0 / D, bias=self.epsc[:, 0:1])
            RECIP(S, st[:, 2:3], st[:, 1:2], [Bst], [Bst])
            yield
            STT(S, "dve", tmp[:], hs[:], st[:, 2:3], G[:], ALU.mult, ALU.mult, [Bhs, Bst, BG], [Btmp])
            yield
            TT(S, "pool", xmb[:], tmp[:], sh[:], ALU.add, [Btmp, Bsh], [Bxmb])
            yield
            for k in range(8):
                TR(S, pT[:, k, :], xmb[:, k * 128:(k + 1) * 128], self.ident[:], [Bxmb, self.Bconst], [BpT])
            CP(S, "act", xmT[:, :, s * 128:(s + 1) * 128], pT[:], [BpT], [BxmT])
            yield

    def epilogue(self, R, ti, bi, s, dst=None):
        S = self.S
        _, _, (GG, BGG) = R.mods
        py, Bpy = R.py[s]
        hs, Bhs = R.hs[bi][s]
        st, Bst = R.st2[s]
        tmp, Btmp = R.tmp
        ho, Bho = R.ho[s]
        ACTV(S, ho[:], py[:], AF.Square, [Bpy], [Bho, Bst], accum_out=st[:, 0:1])
        ACTV(S, st[:, 1:2], st[:, 0:1], AF.Sqrt, [Bst], [Bst], scale=1.0 / D, bias=self.epsc[:, 0:1])
        RECIP(S, st[:, 2:3], st[:, 1:2], [Bst], [Bst])
        STT(S, "dve", tmp[:], py[:], st[:, 2:3], GG[:], ALU.mult, ALU.mult, [Bpy, Bst, BGG], [Btmp])
        TT(S, "pool", ho[:], tmp[:], hs[:], ALU.add, [Btmp, Bhs], [Bho])
        r0 = self.tile_rows(ti, s)
        if dst is None:
            dst_ap = self.H[r0:r0 + 128, :]
        else:
            dst_ap = dst
        S.dma("sp", dst_ap, ho[:], [Bho], [self.db(("H", ti))], Bho)

    def tile_order(self):
        return list(range(1, NT)) + [0]

    def phase_ffn(self, l, f, skip_ctx=False):
        nc, S, I = self.nc, self.S, self.I
        slot = 0 if f == 0 else 2
        with Phase(self) as P:
            w1, Bw1 = P.sb("w1", [128, 8, 2 * DFF], BF16)
            w2, Bw2 = P.sb("w2", [128, NJ, D], BF16)
            for k in range(8):
                S.dma("pool", w1[:, k, :], I["w_ffn_in"][l, f][k * 128:(k + 1) * 128, :], [], [Bw1], Bw1)
            S.dma("pool", w2[:], I["w_ffn_out"][l, f].rearrange("(k p) n -> p k n", p=128), [], [Bw2], Bw2)
            R = self.row_common(P, l, slot)
            hidT, BhidT = P.sb("hidT", [128, NJ, T], BF16)
            sg = [P.sb("sg%d" % i, [128, T], F32) for i in range(2)]
            pg = [P.ps("pg%d" % i, [128, 2, T], F32) for i in range(2)]
            order = self.tile_order()
            if skip_ctx:
                order = order[:-1]
            self.load_mods(R, 0)
            gen = self.prep_steps(R, order[0], 0)
            for _ in gen:
                pass
            for n, ti in enumerate(order):
                bi = n % 2
                xmT, BxmT = R.xmT[bi]
                nxt = None
                if n + 1 < len(order):
                    if order[n + 1] == 0:
                        nxt = None
                    else:
                        nxt = self.prep_steps(R, order[n + 1], 1 - bi)
                for j in range(NJ):
                    pgu, Bpg = pg[j % 2]
                    sgt, Bsg = sg[j % 2]
                    for k in range(8):
                        MM(S, pgu[:, 0, :], w1[:, k, j * 128:(j + 1) * 128], xmT[:, k, :], k == 0, k == 7,
                           [Bw1, BxmT], [Bpg])
                    for k in range(8):
                        MM(S, pgu[:, 1, :], w1[:, k, DFF + j * 128:DFF + (j + 1) * 128], xmT[:, k, :], k == 0, k == 7,
                           [Bw1, BxmT], [Bpg])
                    ACTV(S, sgt[:], pgu[:, 0, :], AF.Silu, [Bpg], [Bsg])
                    TT(S, "dve", hidT[:, j, :], sgt[:], pgu[:, 1, :], ALU.mult, [Bsg, Bpg], [BhidT])
                    if nxt is not None and j % 2 == 1:
                        next(nxt, None)
                if nxt is not None:
                    for _ in nxt:
                        pass
                for s in range(2):
                    py, Bpy = R.py[s]
                    for half in range(2):
                        for j in range(NJ):
                            MM(S, py[:, half * 512:(half + 1) * 512], hidT[:, j, s * 128:(s + 1) * 128],
                               w2[:, j, half * 512:(half + 1) * 512], j == 0, j == NJ - 1, [BhidT, Bw2], [Bpy])
                    self.epilogue(R, ti, bi, s)
                if n + 1 < len(order) and order[n + 1] == 0:
                    self.load_mods(R, 1)
                    for _ in self.prep_steps(R, 0, 1 - bi):
                        pass

    def zcol0(self, ti):
        return ti * T + (1 if ti == 0 else 3)

    def phase_even_in(self, l):
        nc, S, I = self.nc, self.S, self.I
        e = l // 2
        with Phase(self) as P:
            wi, Bwi = P.sb("wi", [128, 8, 2048], BF16)
            S.dma("pool", wi[:], I["w_in_even"][e].rearrange("(k p) n -> p k n", p=128), [], [Bwi], Bwi)
            cd, Bcd = P.sb("cd", [128, 2, 128], BF16)
            S.dma("pool", cd[:], I["CD"], [], [Bcd], Bcd)
            R = self.row_common(P, l, 1, with_epi=False)
            gb = [P.sb("gb%d" % i, [128, 4, T], BF16) for i in range(2)]
            gc, Bgc = P.sb("gc", [128, 4, T], F32)
            zt = [P.sb("zt%d" % i, [128, 4, T], BF16) for i in range(2)]
            xfT, BxfT = P.sb("xfT", [128, 4, T], BF16)
            wt = [P.sb("wt%d" % i, [128, 2, 2, 512], BF16) for i in range(2)]
            pp = [P.ps("pp%d" % i, [128, T], F32) for i in range(2)]
            pw = [P.ps("pw%d" % i, [128, 512], F32) for i in range(2)]
            Gv = self.Gd.rearrange("(cc p) t -> p cc t", p=128)
            Zv = self.Zd.rearrange("(cc p) t -> p cc t", p=128)
            order = self.tile_order()
            self.load_mods(R, 0)
            for _ in self.prep_steps(R, order[0], 0):
                pass
            for n, ti in enumerate(order):
                bi = n % 2
                xmT, BxmT = R.xmT[bi]
                gbt, Bgb = gb[bi]
                ztt, Bzt = zt[bi]
                wtt, Bwt = wt[bi]
                nxt = None
                if n + 1 < len(order) and order[n + 1] != 0:
                    nxt = self.prep_steps(R, order[n + 1], 1 - bi)
                for oc in range(16):
                    pt, Bp = pp[oc % 2]
                    for k in range(8):
                        MM(S, pt[:], wi[:, k, oc * 128:(oc + 1) * 128], xmT[:, k, :], k == 0, k == 7, [Bwi, BxmT], [Bp])
                    if oc < 4:
                        CP(S, "act", gbt[:, oc, :], pt[:], [Bp], [Bgb])
                    elif oc < 8:
                        CP(S, "act", gc[:, oc - 4, :], pt[:], [Bp], [Bgc])
                    elif oc < 12:
                        TT(S, "dve", ztt[:, oc - 8, :], gc[:, oc - 8, :], pt[:], ALU.mult, [Bgc, Bp], [Bzt])
                    else:
                        CP(S, "act", xfT[:, oc - 12, :], pt[:], [Bp], [BxfT])
                    if nxt is not None and oc % 2 == 1:
                        next(nxt, None)
                r0 = ti * T
                S.dma("sp", Gv[:, :, r0:r0 + T], gbt[:], [Bgb], [self.db(("Gd", ti))], Bgb)
                c0 = self.zcol0(ti)
                S.dma("sp", Zv[:, :, c0:c0 + T], ztt[:], [Bzt], [self.db(("Zd", ti))], Bzt)
                for s2 in range(2):
                    for ri in range(2):
                        pwt, Bpw = pw[ri]
                        for g in range(4):
                            MM(S, pwt[:, g * 128:(g + 1) * 128], xfT[:, g, s2 * 128:(s2 + 1) * 128], cd[:, ri, :], True, True,
                               [BxfT, Bcd], [Bpw])
                        CP(S, "dve" if ri == 0 else "act", wtt[:, s2, ri, :], pwt[:], [Bpw], [Bwt])
                for ri in range(2):
                    S.dma("sp", self.Wd[ri, r0:r0 + T, :].rearrange("(s p) n -> p s n", p=128), wtt[:, :, ri, :],
                          [Bwt], [self.db(("Wd", ti))], Bwt)
                if nxt is not None:
                    for _ in nxt:
                        pass
                if n + 1 < len(order) and order[n + 1] == 0:
                    self.load_mods(R, 1)
                    for _ in self.prep_steps(R, 0, 1 - bi):
                        pass

    def phase_f1(self):
        nc, S, I = self.nc, self.S, self.I
        with Phase(self) as P:
            f1, Bf1 = P.sb("f1", [128, 128, 128], BF16)
            for q in range(4):
                S.dma("pool", f1[:, q * 32:(q + 1) * 32, :], I["F1c"][:, q * 32:(q + 1) * 32, :], [], [Bf1], Bf1)
            c256, Bc256 = P.sb("c256", [128, 2, 2, 256], BF16)
            S.dma("pool", c256[:], I["C256"], [], [Bc256], Bc256)
            wst = [P.sb("wst%d" % i, [128, 8, 512], BF16) for i in range(2)]
            bst = [P.sb("bst%d" % i, [128, 8, 512], BF16) for i in range(2)]
            pb = [P.ps("pb%d" % i, [128, 512], F32) for i in range(4)]
            Wall = [self.db(("Wd", ti)) for ti in range(NT)]
            Wl = self.Wd[:, CTX:, :].rearrange("r (t1 t2) n -> r t1 t2 n", t2=128)
            for ch in range(16):
                wt_, Bw = wst[ch % 2]
                bt_, Bb = bst[ch % 2]
                S.dma("sp", [(wt_[ri * 64:(ri + 1) * 64, :, :], Wl[ri, :, ch * 8:(ch + 1) * 8, :]) for ri in range(2)], None,
                      Wall, [Bw], Bw)
                for i in range(8):
                    t2 = ch * 8 + i
                    pt, Bp = pb[i % 4]
                    MM(S, pt[:], f1[:, t2, :], wt_[:, i, :], True, True, [Bf1, Bw], [Bp])
                    CP(S, "act" if i % 2 == 0 else "dve", bt_[:, i, :], pt[:], [Bp], [Bb])
                S.dma("sp", [(self.Bd[ri, :, ch * 8:(ch + 1) * 8, :], bt_[ri * 64:(ri + 1) * 64, :, :]) for ri in range(2)], None,
                      [Bb], [self.db(("Bd", ch))], Bb)
            wc, Bwc = P.sb("wc", [128, 2, 2, 512], BF16)
            yc, Byc = P.sb("yc", [128, 2, 512], BF16)
            S.dma("sp", [(wc[:, :, ri, :], self.Wd[ri, 0:CTX, :].rearrange("(kt p) n -> p kt n", p=128)) for ri in range(2)],
                  None, Wall, [Bwc], Bwc)
            for m in range(2):
                pt, Bp = pb[m]
                n = 0
                for kt in range(2):
                    for ri in range(2):
                        MM(S, pt[:], c256[:, kt, ri, m * 128:(m + 1) * 128], wc[:, kt, ri, :], n == 0, n == 3, [Bc256, Bwc], [Bp])
                        n += 1
                CP(S, "act", yc[:, m, :], pt[:], [Bp], [Byc])
            S.dma("sp", self.Yd[0:CTX, :].rearrange("(m p) n -> p m n", p=128), yc[:], [Byc], [self.db(("Yd", "c"))], Byc)

    def phase_f3(self):
        nc, S, I = self.nc, self.S, self.I
        with Phase(self) as P:
            cs, Bcs = P.sb("cs", [128, 2, 128], BF16)
            S.dma("pool", cs[:], I["C2S2"], [], [Bcs], Bcs)
            btl = [P.sb("bt%d" % i, [128, 2, 8, 512], BF16) for i in range(2)]
            yst = [P.sb("yst%d" % i, [128, 8, 512], BF16) for i in range(2)]
            pb = [P.ps("pc%d" % i, [128, 512], F32) for i in range(4)]
            Ball = [self.db(("Bd", ch)) for ch in range(16)]
            Yl = self.Yd[CTX:, :].rearrange("(t2 t1) n -> t2 t1 n", t1=64)
            for ch in range(8):
                bt_, Bb = btl[ch % 2]
                yt_, By = yst[ch % 2]
                S.dma("sp", [(bt_[:, ri, :, :], self.Bd[ri, ch * 8:(ch + 1) * 8, :, :].rearrange("a t n -> t a n"))
                             for ri in range(2)], None, Ball, [Bb], Bb)
                for i in range(8):
                    pt, Bp = pb[i % 4]
                    MM(S, pt[:], cs[:, 0, :], bt_[:, 0, i, :], True, False, [Bcs, Bb], [Bp])
                    MM(S, pt[:], cs[:, 1, :], bt_[:, 1, i, :], False, True, [Bcs, Bb], [Bp])
                    CP(S, "act" if i % 2 == 0 else "dve", yt_[:, i, :], pt[:], [Bp], [By])
                S.dma("sp", Yl[:, ch * 8:(ch + 1) * 8, :], yt_[:], [By], [self.db(("Yd", "l"))], By)

    def phase_even_out(self, l):
        nc, S, I = self.nc, self.S, self.I
        e = l // 2
        with Phase(self) as P:
            wo, Bwo = P.sb("wo", [128, 8, D], BF16)
            S.dma("pool", wo[:], I["w_out_even"][e].rearrange("(k p) n -> p k n", p=128), [], [Bwo], Bwo)
            cw, Bcw = P.sb("cw", [128, 4, 3], F32)
            for k in range(3):
                S.dma("sp", cw[:, :, k], I["conv_w"][e, k].rearrange("(cc p) -> p cc", p=128), [], [Bcw], Bcw,
                      allow_slow_non_contiguous=True)
            R = self.row_common(P, l, 1, with_prep=False)
            zh = [P.sb("zh%d" % i, [128, 4, T + 2], BF16) for i in range(2)]
            gbl = [P.sb("gbl%d" % i, [128, 4, T], BF16) for i in range(2)]
            yft = [P.sb("yft%d" % i, [128, 2, 512], BF16) for i in range(2)]
            ca, Bca = P.sb("ca", [128, 4, T], F32)
            ycT, BycT = P.sb("ycT", [128, 4, T], BF16)
            yfT, ByfT = P.sb("yfT", [128, 4, T], BF16)
            pT, BpT = R.pT
            Gv = self.Gd.rearrange("(cc p) t -> p cc t", p=128)
            Zv = self.Zd.rearrange("(cc p) t -> p cc t", p=128)
            Zall = [self.db(("Zd", ti)) for ti in range(NT)] + [self.db(("Zd", "pad"))]
            order = self.tile_order()

            def loads(n):
                ti = order[n]
                bi = n % 2
                r0 = ti * T
                c0 = self.zcol0(ti)
                S.dma("sp", zh[bi][0][:], Zv[:, :, c0 - 1:c0 + T + 1], Zall, [zh[bi][1]], zh[bi][1])
                S.dma("sp", gbl[bi][0][:], Gv[:, :, r0:r0 + T], [self.db(("Gd", ti))], [gbl[bi][1]], gbl[bi][1])
                S.dma("sp", yft[bi][0][:], self.Yd[r0:r0 + T, :].rearrange("(s p) n -> p s n", p=128),
                      [self.db(("Yd", "c")), self.db(("Yd", "l"))], [yft[bi][1]], yft[bi][1])
                for s2 in range(2):
                    hs, Bhs = R.hs[bi][s2]
                    S.dma("sp", hs[:], self.H[r0 + s2 * 128:r0 + (s2 + 1) * 128, :], [self.db(("H", "all")), self.db(("H", ti))],
                          [Bhs], Bhs)

            self.load_mods(R, 0)
            loads(0)
            for n, ti in enumerate(order):
                bi = n % 2
                if ti == 0:
                    self.load_mods(R, 1)
                if n + 1 < len(order):
                    loads(n + 1)
                z, Bz = zh[bi]
                g, Bg = gbl[bi]
                yf, Byf = yft[bi]
                for cc in range(4):
                    eng = "dve"
                    TS(S, eng, ca[:, cc, :], z[:, cc, 0:T], cw[:, cc, 0:1], None, ALU.mult, None, [Bz, Bcw], [Bca])
                    STT(S, eng, ca[:, cc, :], z[:, cc, 1:T + 1], cw[:, cc, 1:2], ca[:, cc, :], ALU.mult, ALU.add,
                        [Bz, Bcw, Bca], [Bca])
                    STT(S, eng, ca[:, cc, :], z[:, cc, 2:T + 2], cw[:, cc, 2:3], ca[:, cc, :], ALU.mult, ALU.add,
                        [Bz, Bcw, Bca], [Bca])
                    TT(S, eng, ycT[:, cc, :], ca[:, cc, :], g[:, cc, :], ALU.mult, [Bca, Bg], [BycT])
                for s2 in range(2):
                    for gi in range(4):
                        TR(S, pT[:, s2 * 4 + gi, :], yf[:, s2, gi * 128:(gi + 1) * 128], self.ident[:], [Byf, self.Bconst], [BpT])
                    CP(S, "act", yfT[:, :, s2 * 128:(s2 + 1) * 128], pT[:, s2 * 4:(s2 + 1) * 4, :], [BpT], [ByfT])
                for s2 in range(2):
                    py, Bpy = R.py[s2]
                    for half in range(2):
                        for k in range(8):
                            lh = ycT[:, k, s2 * 128:(s2 + 1) * 128] if k < 4 else yfT[:, k - 4, s2 * 128:(s2 + 1) * 128]
                            MM(S, py[:, half * 512:(half + 1) * 512], lh, wo[:, k, half * 512:(half + 1) * 512], k == 0, k == 7,
                               [BycT, ByfT, Bwo], [Bpy])
                    self.epilogue(R, ti, bi, s2)

    def phase_odd_in(self, l):
        nc, S, I = self.nc, self.S, self.I
        o = l // 2
        last = l == DEPTH - 1
        with Phase(self) as P:
            wio, Bwio = P.sb("wio", [128, 8, 3744], BF16)
            for k in range(8):
                S.dma("pool", wio[:, k, :], I["w_in_odd"][o][k * 128:(k + 1) * 128, :], [], [Bwio], Bwio)
            wsw, Bwsw = P.sb("wsw", [128, 8, 2048 + 32], BF16)
            for (dst0, src0, n, hb) in ((0, 0, 1024, 16), (1024, 1408, 1024, 16), (2048, 3712, 32, 8)):
                for hf in range(2):
                    for k in range(8):
                        dv = wsw[:, k, dst0:dst0 + n].rearrange("p (b h f) -> p b h f", h=2, f=hb)[:, :, hf, :]
                        sv = wio[:, k, src0:src0 + n].rearrange("p (b h f) -> p b h f", h=2, f=hb)[:, :, 1 - hf, :]
                        CP(S, "pool" if k % 2 == 0 else "dve", dv, sv, [Bwio], [Bwsw])
            wuq, Bwuq = P.sb("wuq", [128, 3, 768], BF16)
            S.dma("pool", wuq[:], I["w_uq"][o].rearrange("(k p) n -> p k n", p=128), [], [Bwuq], Bwuq)
            wuqs, Bwuqs = P.sb("wuqs", [128, 3, 768], BF16)
            MEMSET(S, "pool", wuqs[:], 0.0, [Bwuqs])
            for hf in range(2):
                for kk in range(3):
                    dv = wuqs[:, kk, :].rearrange("p (h c) -> p h c", c=96)[:, :, 64:96].rearrange(
                        "p h (a q f) -> p h a q f", a=2, q=2)[:, :, :, hf, :]
                    sv = wuq[:, kk, :].rearrange("p (h c) -> p h c", c=96)[:, :, 64:96].rearrange(
                        "p h (a q f) -> p h a q f", a=2, q=2)[:, :, :, 1 - hf, :]
                    CP(S, "pool", dv, sv, [Bwuq], [Bwuqs])
            wuk, Bwuk = P.sb("wuk", [128, 2, 512], BF16)
            S.dma("pool", wuk[:], I["w_uk"][o].rearrange("(k p) n -> p k n", p=128), [], [Bwuk], Bwuk)
            wuv, Bwuv = P.sb("wuv", [128, 2, 512], BF16)
            S.dma("pool", wuv[:], I["w_uv"][o].rearrange("(k p) n -> p k n", p=128), [], [Bwuv], Bwuv)
            gq, Bgq = P.sb("gq", [128, 384], F32)
            S.dma("sp", gq[:], I["g_q_mla"][o:o + 1, :].partition_broadcast(128), [], [Bgq], Bgq)
            gkv, Bgkv = P.sb("gkv", [128, 256], F32)
            S.dma("sp", gkv[:], I["g_kv_mla"][o:o + 1, :].partition_broadcast(128), [], [Bgkv], Bgkv)
            R = self.row_common(P, l, 1, with_epi=False)
            cs = [P.sb("rda%d" % i, [128, 2, T], F32) for i in range(2)]
            csm = [P.sb("rm%d" % i, [96, 2, T], F32) for i in range(2)]
            csr = [P.sb("rr%d" % i, [32, 2, T], F32) for i in range(2)]
            t1, Bt1 = P.sb("t1", [128, T], F32)
            t2, Bt2 = P.sb("t2", [128, T], F32)
            qall = [P.sb("qall%d" % i, [128, 8, T], BF16) for i in range(1)]
            kall = [P.sb("kall%d" % i, [128, 8, T], BF16) for i in range(1)]
            qmall = [P.sb("qmall%d" % i, [96, 8, T], BF16) for i in range(1)]
            kmall = [P.sb("kmall%d" % i, [64, 8, T], BF16) for i in range(1)]
            krt = [P.sb("krt%d" % i, [32, T], BF16) for i in range(2)]
            vt = [P.sb("vt%d" % i, [128, 2, 1024], BF16) for i in range(2)]
            vmt = [P.sb("vmt%d" % i, [128, 2, 512], BF16) for i in range(2)]
            cqn, Bcqn = P.sb("cqn", [128, 384], BF16)
            ckn, Bckn = P.sb("ckn", [128, 256], BF16)
            cqT, BcqT = P.sb("cqT", [128, 3, T], BF16)
            ckT, BckT = P.sb("ckT", [128, 2, T], BF16)
            stq, Bstq = P.sb("stq", [128, 8], F32)
            junk, Bjunk = P.sb("junk", [128, 384], F32)
            pg = [P.ps("pgn%d" % i, [128, 512], F32) for i in range(6)]
            pT, BpT = R.pT
            self._pgi = 0

            def nextp():
                self._pgi = (self._pgi + 1) % 6
                return pg[self._pgi]

            order = self.tile_order()
            self.load_mods(R, 0)
            for _ in self.prep_steps(R, order[0], 0):
                pass

            def rope_out(p1, Bp1, p2, Bp2, cosap, sinap, Bcs, outap, Bout, np_):
                TT(S, "dve", t1[:np_, :], p1, cosap, ALU.mult, [Bp1, Bcs], [Bt1])
                CP(S, "act", t2[:np_, :], p2, [Bp2], [Bt2])
                TT(S, "pool", t2[:np_, :], t2[:np_, :], sinap, ALU.mult, [Bt2, Bcs], [Bt2])
                TT(S, "pool", outap, t1[:np_, :], t2[:np_, :], ALU.add, [Bt1, Bt2], [Bout])

            for n, ti in enumerate(order):
                bi = n % 2
                r0 = ti * T
                xmT, BxmT = R.xmT[bi]
                nxt = None
                if n + 1 < len(order) and order[n + 1] != 0:
                    nxt = self.prep_steps(R, order[n + 1], 1 - bi)
                (cst, Bcs), (csmt, Bcsm), (csrt, Bcsr) = cs[bi], csm[bi], csr[bi]
                S.dma("sp", cst[:], I["rope_da"][:, :, r0:r0 + T].rearrange("c p t -> p c t"), [], [Bcs], Bcs)
                S.dma("sp", csmt[:], I["rope_m"][:, :, r0:r0 + T].rearrange("c p t -> p c t"), [], [Bcsm], Bcsm)
                S.dma("sp", csrt[:], I["rope_m"][:, 64:96, r0:r0 + T].rearrange("c p t -> p c t"), [], [Bcsr], Bcsr)
                (qa, Bqa), (ka, Bka), (qm, Bqm), (km, Bkm), (kr, Bkr) = qall[0], kall[0], qmall[0], kmall[0], krt[bi]
                (vtt, Bvt), (vmtt, Bvmt) = vt[bi], vmt[bi]
                for (c0, sw0, dst, Bdst) in ((0, 0, qa, Bqa), (1408, 1024, ka, Bka)):
                    if last and ti == 0 and c0 == 0:
                        continue
                    for h in range(8):
                        p1, Bp1 = nextp()
                        p2, Bp2 = nextp()
                        for k in range(8):
                            MM(S, p1[:, :T], wio[:, k, c0 + h * 128:c0 + (h + 1) * 128], xmT[:, k, :], k == 0, k == 7,
                               [Bwio, BxmT], [Bp1])
                        for k in range(8):
                            MM(S, p2[:, :T], wsw[:, k, sw0 + h * 128:sw0 + (h + 1) * 128], xmT[:, k, :], k == 0, k == 7,
                               [Bwsw, BxmT], [Bp2])
                        rope_out(p1[:, :T], Bp1, p2[:, :T], Bp2, cst[:, 0, :], cst[:, 1, :], Bcs, dst[:, h, :], Bdst, 128)
                        if nxt is not None and h % 2 == 1:
                            next(nxt, None)
                if not (last and ti == 0):
                    S.dma("sp", self.Qda.rearrange("h p t -> p h t")[:, :, r0:r0 + T], qa[:], [Bqa], [self.db(("Qda", ti))], Bqa)
                S.dma("sp", self.Kda.rearrange("h p t -> p h t")[:, :, r0:r0 + T], ka[:], [Bka], [self.db(("Kda", ti))], Bka)
                p1, Bp1 = nextp()
                p2, Bp2 = nextp()
                for k in range(8):
                    MM(S, p1[:32, :T], wio[:, k, 3712:3744], xmT[:, k, :], k == 0, k == 7, [Bwio, BxmT], [Bp1])
                for k in range(8):
                    MM(S, p2[:32, :T], wsw[:, k, 2048:2080], xmT[:, k, :], k == 0, k == 7, [Bwsw, BxmT], [Bp2])
                rope_out(p1[:32, :T], Bp1, p2[:32, :T], Bp2, csrt[:, 0, :], csrt[:, 1, :], Bcsr, kr[:], Bkr, 32)
                for h in range(8):
                    S.dma("sp", self.Km[h, 64:96, r0:r0 + T], kr[:], [Bkr], [self.db(("Km", ti))], Bkr)
                for s2 in range(2):
                    sl = slice(s2 * 128, (s2 + 1) * 128)
                    for half in range(2):
                        pv, Bpv = nextp()
                        for k in range(8):
                            MM(S, pv[:], xmT[:, k, sl], wio[:, k, 2432 + half * 512:2432 + (half + 1) * 512], k == 0, k == 7,
                               [Bwio, BxmT], [Bpv])
                        CP(S, "act" if half == 0 else "dve", vtt[:, s2, half * 512:(half + 1) * 512], pv[:], [Bpv], [Bvt])
                    for (c0, wd, gt, Bg, dstn, Bdn, dT, BdT, nk, col) in (
                            (1024, 384, gq, Bgq, cqn, Bcqn, cqT, BcqT, 3, 0), (3456, 256, gkv, Bgkv, ckn, Bckn, ckT, BckT, 2, 4)):
                        if last and ti == 0 and c0 == 1024:
                            continue
                        pc, Bpc = nextp()
                        for k in range(8):
                            MM(S, pc[:, :wd], xmT[:, k, sl], wio[:, k, c0:c0 + wd], k == 0, k == 7, [Bwio, BxmT], [Bpc])
                        ACTV(S, junk[:, :wd], pc[:, :wd], AF.Square, [Bpc], [Bjunk, Bstq], accum_out=stq[:, col:col + 1])
                        ACTV(S, stq[:, col + 1:col + 2], stq[:, col:col + 1], AF.Sqrt, [Bstq], [Bstq], scale=1.0 / wd,
                             bias=self.epsc[:, 0:1])
                        RECIP(S, stq[:, col + 2:col + 3], stq[:, col + 1:col + 2], [Bstq], [Bstq])
                        STT(S, "dve", dstn[:], pc[:, :wd], stq[:, col + 2:col + 3], gt[:], ALU.mult, ALU.mult,
                            [Bpc, Bstq, Bg], [Bdn])
                        for kk in range(nk):
                            TR(S, pT[:, kk, :], dstn[:, kk * 128:(kk + 1) * 128], self.ident[:], [Bdn, self.Bconst], [BpT])
                        CP(S, "act", dT[:, :, sl], pT[:, 0:nk, :], [BpT], [BdT])
                S.dma("sp", self.Vda[r0:r0 + T, :].rearrange("(s p) n -> p s n", p=128), vtt[:], [Bvt], [self.db(("Vda", ti))], Bvt)
                if not (last and ti == 0):
                    for h in range(8):
                        p1, Bp1 = nextp()
                        p2, Bp2 = nextp()
                        for kk in range(3):
                            MM(S, p1[:96, :T], wuq[:, kk, h * 96:(h + 1) * 96], cqT[:, kk, :], kk == 0, kk == 2, [Bwuq, BcqT], [Bp1])
                        for kk in range(3):
                            MM(S, p2[:96, :T], wuqs[:, kk, h * 96:(h + 1) * 96], cqT[:, kk, :], kk == 0, kk == 2, [Bwuqs, BcqT], [Bp2])
                        rope_out(p1[:96, :T], Bp1, p2[:96, :T], Bp2, csmt[:, 0, :], csmt[:, 1, :], Bcsm, qm[:, h, :], Bqm, 96)
                    S.dma("sp", self.Qm.rearrange("h p t -> p h t")[:, :, r0:r0 + T], qm[:], [Bqm], [self.db(("Qm", ti))], Bqm)
                for h in range(8):
                    p1, Bp1 = nextp()
                    for kk in range(2):
                        MM(S, p1[:64, :T], wuk[:, kk, h * 64:(h + 1) * 64], ckT[:, kk, :], kk == 0, kk == 1, [Bwuk, BckT], [Bp1])
                    CP(S, "act" if h % 2 == 0 else "dve", km[:, h, :], p1[:64, :T], [Bp1], [Bkm])
                S.dma("sp", self.Km.rearrange("h p t -> p h t")[0:64, :, r0:r0 + T], km[:], [Bkm], [self.db(("Km", ti))], Bkm)
                for s2 in range(2):
                    sl = slice(s2 * 128, (s2 + 1) * 128)
                    pv, Bpv = nextp()
                    for kk in range(2):
                        MM(S, pv[:], ckT[:, kk, sl], wuv[:, kk, :], kk == 0, kk == 1, [Bwuv, BckT], [Bpv])
                    CP(S, "act" if s2 == 0 else "dve", vmtt[:, s2, :], pv[:], [Bpv], [Bvmt])
                S.dma("sp", self.Vm[r0:r0 + T, :].rearrange("(s p) n -> p s n", p=128), vmtt[:], [Bvmt], [self.db(("Vm", ti))], Bvmt)
                if nxt is not None:
                    for _ in nxt:
                        pass
                if n + 1 < len(order) and order[n + 1] == 0:
                    self.load_mods(R, 1)
                    for _ in self.prep_steps(R, 0, 1 - bi):
                        pass

    def phase_attn(self, l):
        nc, S, I = self.nc, self.S, self.I
        o = l // 2
        last = l == DEPTH - 1
        lam_init = 0.8 - 0.6 * math.exp(-0.3 * l)
        NKT = NTOK // 128
        with Phase(self) as P:
            onesb, Bones = P.sb("onesb", [128, 128], BF16)
            MEMSET(S, "pool", onesb[:], 1.0, [Bones])
            onesf, Bonesf = P.sb("onesf", [128, 128], F32)
            MEMSET(S, "pool", onesf[:], 1.0, [Bonesf])
            lv, Blv = P.sb("lv", [128, 4 * 64], F32)
            S.dma("sp", lv[:], I["lamv"][o:o + 1, :].partition_broadcast(128), [], [Blv], Blv)
            lw, Blw = P.sb("lw", [128, 16], F32)
            pr, Bpr = P.sb("pr", [128, 2, 64], F32)
            for i in range(2):
                TT(S, "dve", pr[:, i, :], lv[:, (2 * i) * 64:(2 * i + 1) * 64], lv[:, (2 * i + 1) * 64:(2 * i + 2) * 64], ALU.mult,
                   [Blv], [Bpr])
                S.op("dve", lambda e, i=i: e.reduce_sum(out=lw[:, i:i + 1], in_=pr[:, i, :], axis=AX.X), [Bpr], [Blw])
                ACTV(S, lw[:, 2 + i:3 + i], lw[:, i:i + 1], AF.Exp, [Blw], [Blw])
            TT(S, "dve", lw[:, 4:5], lw[:, 3:4], lw[:, 2:3], ALU.subtract, [Blw], [Blw])
            TS(S, "dve", lw[:, 5:6], lw[:, 4:5], -lam_init, None, ALU.add, None, [Blw], [Blw])
            gs, Bgs = P.sb("gs", [128, 2], F32)
            S.dma("sp", gs[:, 0:1], I["g_subln"][o].rearrange("(p o) -> p o", o=1), [], [Bgs], Bgs)
            TS(S, "dve", gs[:, 1:2], gs[:, 0:1], 1.0 - lam_init, None, ALU.mult, None, [Bgs], [Bgs])
            KT = [P.sb("KT%d" % i, [128, NTOK], BF16) for i in range(2)]
            QT = [P.sb("QT%d" % i, [128, NTOK], BF16) for i in range(2)]
            VV = [P.sb("VV%d" % i, [128, NKT, 128], BF16) for i in range(2)]
            PT = [P.sb("PT%d" % i, [128, 512], BF16) for i in range(3)]
            om = [P.sb("om%d" % i, [128, 512], F32) for i in range(2)]
            rs, Brs = P.sb("rs", [128, 512], F32)
            oo, Boo = P.sb("oo", [128, 512], F32)
            sq, Bsq = P.sb("sq", [128, 512], F32)
            rstd, Brstd = P.sb("rstd", [128, 512], F32)
            on = [P.sb("on%d" % i, [128, 512], BF16) for i in range(2)]
            pS = [P.ps("pS%d" % i, [128, 512], F32) for i in range(3)]
            po, Bpo = P.ps("po", [128, 512], F32)
            psm, Bpsm = P.ps("psm", [128, 512], F32)
            pss, Bpss = P.ps("pss", [128, 512], F32)
            self._cnt = 0
            Qall = {"da": [self.db(("Qda", ti)) for ti in range(NT)], "m": [self.db(("Qm", ti)) for ti in range(NT)]}

            def attend(kt_ap, qt_ap, v_ap, dv, scale, nkt, ncols, dst, Bdst):
                for kt in range(nkt):
                    c = self._cnt
                    self._cnt += 1
                    ps_, Bps = pS[c % 3]
                    pt_, Bpt = PT[c % 3]
                    lh, rd1 = kt_ap(kt)
                    rh, rd2 = qt_ap
                    MM(S, ps_[:, :ncols], lh, rh, True, True, [rd1, rd2], [Bps])
                    ACTV(S, pt_[:, :ncols], ps_[:, :ncols], AF.Exp, [Bps], [Bpt], scale=scale)
                    va, rd3 = v_ap(kt)
                    MM(S, po[:dv, :ncols], va, pt_[:, :ncols], kt == 0, kt == nkt - 1, [rd3, Bpt], [Bpo])
                    MM(S, psm[:dv, :ncols], onesb[:, :dv], pt_[:, :ncols], kt == 0, kt == nkt - 1, [Bones, Bpt], [Bpsm])
                RECIP(S, rs[:dv, :ncols], psm[:dv, :ncols], [Bpsm], [Brs])
                TT(S, "dve", dst, po[:dv, :ncols], rs[:dv, :ncols], ALU.mult, [Bpo, Brs], [Bdst])

            qtiles = [(CTX + i * 512, 512, NKT) for i in range(SEQ // 512)]
            if not last:
                qtiles.append((0, CTX, CTX // 128))
            units = [("da", h) for h in range(8)] + [("m", h) for h in range(8)]

            def load_unit(u, bi):
                kind, h = u
                (kt_, Bk), (qt_, Bq), (vv_, Bv) = KT[bi], QT[bi], VV[bi]
                if kind == "da":
                    S.dma("sp", kt_[:], self.Kda[h], [self.db(("Kda", ti)) for ti in range(NT)], [Bk], Bk)
                    S.dma("sp", qt_[:], self.Qda[h], Qall["da"], [Bq], Bq)
                    S.dma("sp", vv_[:], self.Vda[:, h * 128:(h + 1) * 128].rearrange("(kt p) d -> p kt d", p=128),
                          [self.db(("Vda", ti)) for ti in range(NT)], [Bv], Bv)
                else:
                    S.dma("sp", kt_[:96, :], self.Km[h], [self.db(("Km", ti)) for ti in range(NT)], [Bk], Bk)
                    S.dma("sp", qt_[:96, :], self.Qm[h], Qall["m"], [Bq], Bq)
                    S.dma("sp", vv_[:, :, 0:64], self.Vm[:, h * 64:(h + 1) * 64].rearrange("(kt p) d -> p kt d", p=128),
                          [self.db(("Vm", ti)) for ti in range(NT)], [Bv], Bv)

            load_unit(units[0], 0)
            for ui, u in enumerate(units):
                bi = ui % 2
                kind, h = u
                if ui + 1 < len(units):
                    load_unit(units[ui + 1], 1 - bi)
                (kt_, Bk), (qt_, Bq), (vv_, Bv) = KT[bi], QT[bi], VV[bi]
                for qi, (c0, ncols, nkt) in enumerate(qtiles):
                    ont, Bon = on[qi % 2]
                    if kind == "da":
                        for m in range(2):
                            pr_ = slice(m * 64, (m + 1) * 64)
                            attend(lambda kt, pr_=pr_: (kt_[pr_, kt * 128:(kt + 1) * 128], Bk), (qt_[pr_, c0:c0 + ncols], Bq),
                                   lambda kt: (vv_[:, kt, :], Bv), 128, 0.125, nkt, ncols, om[m][0][:, :ncols], om[m][1])
                        STT(S, "dve", oo[:, :ncols], om[1][0][:, :ncols], lw[:, 5:6], om[0][0][:, :ncols], ALU.mult, ALU.add,
                            [om[0][1], om[1][1], Blw], [Boo])
                        ACTV(S, sq[:, :ncols], oo[:, :ncols], AF.Square, [Boo], [Bsq])
                        MM(S, pss[:, :ncols], onesf[:], sq[:, :ncols], True, True, [Bonesf, Bsq], [Bpss])
                        ACTV(S, rstd[:, :ncols], pss[:, :ncols], AF.Sqrt, [Bpss], [Brstd], scale=1.0 / 128, bias=self.epsc[:, 0:1])
                        RECIP(S, rstd[:, :ncols], rstd[:, :ncols], [Brstd], [Brstd])
                        STT(S, "dve", ont[:, :ncols], oo[:, :ncols], gs[:, 1:2], rstd[:, :ncols], ALU.mult, ALU.mult,
                            [Boo, Bgs, Brstd], [Bon])
                        S.dma("sp", self.Od[h * 128:(h + 1) * 128, c0:c0 + ncols], ont[:, :ncols], [Bon],
                              [self.db(("Od", ui, qi))], Bon)
                    else:
                        attend(lambda kt: (kt_[:96, kt * 128:(kt + 1) * 128], Bk), (qt_[:96, c0:c0 + ncols], Bq),
                               lambda kt: (vv_[:, kt, 0:64], Bv), 64, 96 ** -0.5, nkt, ncols, om[0][0][:64, :ncols], om[0][1])
                        CP(S, "pool", ont[:64, :ncols], om[0][0][:64, :ncols], [om[0][1]], [Bon])
                        S.dma("sp", self.Od[1024 + h * 64:1024 + (h + 1) * 64, c0:c0 + ncols], ont[:64, :ncols], [Bon],
                              [self.db(("Od", ui, qi))], Bon)

    def phase_odd_out(self, l):
        nc, S, I = self.nc, self.S, self.I
        o = l // 2
        last = l == DEPTH - 1
        with Phase(self) as P:
            wo, Bwo = P.sb("woo", [128, 12, D], BF16)
            S.dma("pool", wo[:], I["w_out_odd"][o].rearrange("(k p) n -> p k n", p=128), [], [Bwo], Bwo)
            R = self.row_common(P, l, 1, with_prep=False)
            oT = [P.sb("oT%d" % i, [128, 12, T], BF16) for i in range(2)]
            order = self.tile_order()
            if last:
                order = order[:-1]
            Ov = self.Od.rearrange("(k p) t -> p k t", p=128)

            def loads(n):
                ti = order[n]
                bi = n % 2
                r0 = ti * T
                S.dma("sp", oT[bi][0][:], Ov[:, :, r0:r0 + T], [], [oT[bi][1]], oT[bi][1])
                for s2 in range(2):
                    hs, Bhs = R.hs[bi][s2]
                    S.dma("sp", hs[:], self.H[r0 + s2 * 128:r0 + (s2 + 1) * 128, :], [self.db(("H", "all")), self.db(("H", ti))],
                          [Bhs], Bhs)

            self.load_mods(R, 0)
            loads(0)
            for n, ti in enumerate(order):
                bi = n % 2
                if ti == 0:
                    self.load_mods(R, 1)
                if n + 1 < len(order):
                    loads(n + 1)
                ot, Bot = oT[bi]
                for s2 in range(2):
                    py, Bpy = R.py[s2]
                    for half in range(2):
                        for k in range(12):
                            MM(S, py[:, half * 512:(half + 1) * 512], ot[:, k, s2 * 128:(s2 + 1) * 128],
                               wo[:, k, half * 512:(half + 1) * 512], k == 0, k == 11, [Bot, Bwo], [Bpy])
                    self.epilogue(R, ti, bi, s2)


_CACHE = {}


def _get_nc(stop_after=None):
    if stop_after not in _CACHE:
        kb = KB(stop_after)
        kb.build()
        _CACHE[stop_after] = kb
    return _CACHE[stop_after]


def _host_consts():
    two_pi = 2.0 * np.pi
    c = np.arange(128)
    ang = two_pi * np.outer(c, c) / 128.0
    CD = np.stack([np.cos(ang), -np.sin(ang)], 1) / np.sqrt(128.0)
    C2S2 = np.stack([np.cos(ang), np.sin(ang)], 1) / np.sqrt(128.0)
    t1 = np.arange(64)[:, None, None]; t2 = np.arange(128)[None, :, None]; t1p = np.arange(64)[None, None, :]
    th = two_pi * (t1p * (128 * t1 + t2) % 8192) / 8192.0
    Ere = np.cos(th) / 8.0; Eim = -np.sin(th) / 8.0
    F1 = np.zeros((2, 64, 128, 2, 64))
    F1[0, :, :, 0, :] = Ere; F1[0, :, :, 1, :] = Eim
    F1[1, :, :, 0, :] = -Eim; F1[1, :, :, 1, :] = Ere
    F1c = F1.reshape(128, 128, 128)
    t = np.arange(256); a = two_pi * (np.outer(t, t) % 256) / 256.0
    C256 = np.stack([np.cos(a), np.sin(a)], 1) / 16.0
    C256 = C256.reshape(2, 128, 2, 256).transpose(1, 0, 2, 3)
    g = lambda a: np.ascontiguousarray(a.astype(np.float32))
    pos = np.arange(SEQ)
    posa = np.stack([pos // 64, pos % 64], 0).astype(np.float64)

    def rope_tab(rot):
        nf = rot // 4
        fr = 10000.0 ** (-np.arange(nf, dtype=np.float32).astype(np.float64) / nf)
        d = np.arange(rot)
        a = d // (rot // 2); hf = (d % (rot // 2)) // nf; f = d % nf
        ang = posa[a, :] * fr[f][:, None]
        cs = np.cos(ang); sn = np.sin(ang) * np.where(hf == 0, -1.0, 1.0)[:, None]
        return cs, sn
    cda, sda = rope_tab(64)
    rope_da = np.zeros((2, 128, NTOK)); rope_da[0, :, :CTX] = 1.0
    rope_da[0, :, CTX:] = np.tile(cda, (2, 1)); rope_da[1, :, CTX:] = np.tile(sda, (2, 1))
    cm, sm = rope_tab(32)
    rope_m = np.zeros((2, 96, NTOK)); rope_m[0] = 1.0
    rope_m[0, 64:, CTX:] = cm; rope_m[1, 64:, CTX:] = sm
    return {"CD": g(CD), "F1c": g(F1c), "C2S2": g(C2S2), "C256": g(C256), "rope_da": g(rope_da), "rope_m": g(rope_m)}


def make_in_maps(inputs):
    n = 8
    f = lambda a: np.ascontiguousarray(np.asarray(a, dtype=np.float32))
    x = f(inputs["x"]); c = f(inputs["c"]); ctx = f(inputs["ctx"]); c_ctx = f(inputs["c_ctx"])
    shared = {
        "w_mod": f(inputs["w_mod"]), "b_mod": f(inputs["b_mod"]),
        "norm_g": f(inputs["norm_g"]).reshape(DEPTH, 6 * D),
        "w_ffn_in": f(inputs["w_ffn_in"]), "w_ffn_out": f(inputs["w_ffn_out"]),
    }
    shared.update(_host_consts())
    shared["w_in_even"] = f(inputs["w_in_even"]); shared["conv_w"] = f(inputs["conv_w"])
    shared["w_out_even"] = f(inputs["w_out_even"])
    for k_ in ("w_in_odd", "g_q_mla", "w_uq", "g_kv_mla", "w_uk", "w_uv", "g_subln", "w_out_odd"):
        shared[k_] = f(inputs[k_])
    shared["lamv"] = np.ascontiguousarray(np.concatenate(
        [f(inputs["lam_q1"]), f(inputs["lam_k1"]), f(inputs["lam_q2"]), f(inputs["lam_k2"])], axis=1))
    maps = []
    for b in range(n):
        m = dict(shared)
        m["x"] = x[b]
        m["ctx"] = ctx[b]
        m["cc"] = np.ascontiguousarray(np.stack([c[b], c_ctx], 0))
        maps.append(m)
    return maps


def kernel(**inputs):
    from concourse.bass_utils import run_bass_kernel_spmd
    kb = _get_nc(None)
    maps = make_in_maps(inputs)
    res = run_bass_kernel_spmd(kb.nc, maps, core_ids=list(range(8)))
    return np.stack([r["out"] for r in res.results], 0)
```
